# Optimizing a Trainium2 kernel written in Bass

```python
import math
import jax
import jax.numpy as jnp
from jax import lax
import numpy as np

D_MODEL = 2048
BATCH = 4
SEQ = 4096
DEPTH = 2

HEAD_DIM = 128
A_HEADS = 6
B_HEADS = 4
C_HEADS = 6
B_QK_DIM = 64
DILATED_CONFIGS = ((128, 1), (512, 4), (2048, 16))
Q_BLOCK = 128
MLSTM_CHUNK = 128
CONV_WIDTH = 3
MEM_LEN = 256
X_HEADS = 4
X_HEAD_DIM = D_MODEL // X_HEADS
D_FF = 4 * D_MODEL
FORGET_BIAS = 3.0
EPS = 1e-6
NEG = -1e30

A_W = A_HEADS * HEAD_DIM
B_QK_W = B_HEADS * 2 * B_QK_DIM
B_V_W = B_HEADS * HEAD_DIM
C_W = C_HEADS * HEAD_DIM
N_GATES = 4 * C_HEADS
N_MIX_HEADS = A_HEADS + B_HEADS + C_HEADS
MIX_W = N_MIX_HEADS * HEAD_DIM
IN_SPLITS = (A_W, A_W, A_W, B_QK_W, B_QK_W, B_V_W, 2 * C_W, C_W, C_W, N_GATES)
IN_W = 3 * A_W + 2 * B_QK_W + B_V_W + 4 * C_W + N_GATES
F32 = jnp.float32

kernel_name = 'hybrid_dilated_diff_mlstm_encoder'


def rmsnorm(x, g):
    xf = x.astype(F32)
    y = xf * lax.rsqrt(jnp.mean(xf * xf, axis=-1, keepdims=True) + EPS)
    return (y * g.astype(F32)).astype(x.dtype)


def alibi_slopes(n_heads):
    return jnp.asarray(2.0 ** (-8.0 * np.arange(1, n_heads + 1) / n_heads), dtype=F32)


def split_heads(u, n_heads):
    b, t, _ = u.shape
    return u.reshape(b, t, n_heads, -1).transpose(0, 2, 1, 3)


def centred_conv(u, w):
    k, t = w.shape[0], u.shape[1]
    left = k // 2
    up = jnp.pad(u, ((0, 0), (left, k - 1 - left), (0, 0)))
    out = up[:, 0:t] * w[0]
    for j in range(1, k):
        out = out + up[:, j:j + t] * w[j]
    return out


def dilated_window_attn(q, k, v, slopes, window, dilation):
    b, h, t, dh = q.shape
    r = dilation
    half = window // (2 * r)
    n_sub = t // r
    nb = -(-n_sub // half)
    lp = nb * half

    def to_sub(u):
        return u.reshape(b, h, n_sub, r, dh).transpose(0, 1, 3, 2, 4)

    qb = jnp.pad(to_sub(q), ((0, 0), (0, 0), (0, 0), (0, lp - n_sub), (0, 0))).reshape(b, h, r, nb, half, dh)

    def band(u):
        up = jnp.pad(to_sub(u), ((0, 0), (0, 0), (0, 0), (half, lp - n_sub + half), (0, 0)))
        up = up.reshape(b, h, r, nb + 2, half, dh)
        return jnp.concatenate([up[:, :, :, 0:nb], up[:, :, :, 1:nb + 1], up[:, :, :, 2:nb + 2]], axis=4)

    kb, vb = band(k), band(v)
    qi = jnp.arange(nb)[:, None] * half + jnp.arange(half)[None, :]
    ki = jnp.arange(nb)[:, None] * half + jnp.arange(3 * half)[None, :] - half
    delta = jnp.abs(qi[:, :, None] - ki[:, None, :])
    valid = (delta <= half) & (ki[:, None, :] >= 0) & (ki[:, None, :] < n_sub)
    dist = (delta * r).astype(F32)
    s = jnp.einsum('bhrnqd,bhrnkd->bhrnqk', qb, kb).astype(F32) * (dh ** -0.5)
    s = s - slopes[None, :, None, None, None, None] * dist
    s = jnp.where(valid, s, NEG)
    m = jnp.max(s, axis=-1, keepdims=True)
    p = jnp.exp(s - m)
    den = jnp.sum(p, axis=-1)
    o = jnp.einsum('bhrnqk,bhrnkd->bhrnqd', p.astype(v.dtype), vb).astype(F32) / den[..., None]
    lse = m[..., 0] + jnp.log(den)
    o = o.reshape(b, h, r, lp, dh)[:, :, :, :n_sub].transpose(0, 1, 3, 2, 4).reshape(b, h, t, dh)
    lse = lse.reshape(b, h, r, lp)[:, :, :, :n_sub].transpose(0, 1, 3, 2).reshape(b, h, t)
    return o, lse


def dilated_mixture(q, k, v, slopes):
    outs, lses = [], []
    for window, dilation in DILATED_CONFIGS:
        o, lse = dilated_window_attn(q, k, v, slopes, window, dilation)
        outs.append(o)
        lses.append(lse)
    wts = jax.nn.softmax(jnp.stack(lses, axis=0), axis=0)
    return jnp.einsum('gbht,gbhtd->bhtd', wts, jnp.stack(outs, axis=0))


def diff_attention(q1, q2, k1, k2, v, lam, slopes):
    b, h, t, d = q1.shape
    nq = t // Q_BLOCK
    scale = d ** -0.5
    kpos = jnp.arange(t)

    def blocks(u):
        return u.reshape(b, h, nq, Q_BLOCK, d).transpose(2, 0, 1, 3, 4)

    def one_block(args):
        qa, qc, start = args
        qpos = start + jnp.arange(Q_BLOCK)
        bias = -slopes[:, None, None] * jnp.abs(qpos[:, None] - kpos[None, :]).astype(F32)
        s1 = jnp.einsum('bhqd,bhkd->bhqk', qa, k1).astype(F32) * scale + bias
        s2 = jnp.einsum('bhqd,bhkd->bhqk', qc, k2).astype(F32) * scale + bias
        a = jax.nn.softmax(s1, axis=-1) - lam * jax.nn.softmax(s2, axis=-1)
        return jnp.einsum('bhqk,bhkd->bhqd', a.astype(v.dtype), v)

    o = lax.map(one_block, (blocks(q1), blocks(q2), jnp.arange(nq) * Q_BLOCK))
    return o.transpose(1, 2, 0, 3, 4).reshape(b, h, t, v.shape[-1])


def mlstm_chunkwise(q, k, v, log_i, log_f):
    b, nh, t, d = q.shape
    lc = MLSTM_CHUNK
    nc = t // lc

    def chunks(u):
        return jnp.moveaxis(u.reshape((b, nh, nc, lc) + u.shape[3:]), 2, 0)

    q = q.astype(F32) * (d ** -0.5)
    k = k.astype(F32)
    v = v.astype(F32)
    lower = jnp.tril(jnp.ones((lc, lc), dtype=bool))

    def step(carry, xs):
        c_state, n_state, m_state = carry
        qc, kc, vc, li, lf = xs
        bcum = jnp.cumsum(lf, axis=-1)
        dmat = jnp.where(lower, bcum[..., :, None] - bcum[..., None, :] + li[..., None, :], NEG)
        inter = bcum + m_state[..., None]
        m_t = jnp.maximum(inter, jnp.max(dmat, axis=-1))
        dw = jnp.exp(dmat - m_t[..., None])
        iw = jnp.exp(inter - m_t)
        sc = jnp.einsum('bhtd,bhsd->bhts', qc, kc) * dw
        num = iw[..., None] * jnp.einsum('bhtd,bhde->bhte', qc, c_state) + jnp.einsum('bhts,bhse->bhte', sc, vc)
        den = iw * jnp.einsum('bhtd,bhd->bht', qc, n_state) + jnp.sum(sc, axis=-1)
        h_out = num / jnp.maximum(jnp.abs(den), jnp.exp(-m_t))[..., None]
        b_last = bcum[..., -1]
        g = b_last[..., None] - bcum + li
        m_new = jnp.maximum(b_last + m_state, jnp.max(g, axis=-1))
        decay = jnp.exp(b_last + m_state - m_new)
        gw = jnp.exp(g - m_new[..., None])
        c_new = decay[..., None, None] * c_state + jnp.einsum('bhs,bhsd,bhse->bhde', gw, kc, vc)
        n_new = decay[..., None] * n_state + jnp.einsum('bhs,bhsd->bhd', gw, kc)
        return (c_new, n_new, m_new), h_out

    init = (jnp.zeros((b, nh, d, d), F32), jnp.zeros((b, nh, d), F32), jnp.zeros((b, nh), F32))
    _, hs = lax.scan(step, init, (chunks(q), chunks(k), chunks(v), chunks(log_i), chunks(log_f)))
    return jnp.moveaxis(hs, 0, 2).reshape(b, nh, t, d)


def bidirectional_mlstm(q, k, v, li_f, lf_f, li_b, lf_b):
    fwd = mlstm_chunkwise(q, k, v, li_f, lf_f)
    flip = lambda u: jnp.flip(u, axis=2)
    bwd = flip(mlstm_chunkwise(flip(q), flip(k), flip(v), flip(li_b), flip(lf_b)))
    return fwd + bwd


def hybrid_mixer(hn, layer, w_in, conv_w, gate_b, diff_lambda, head_norm_g, w_out):
    b, t, _ = hn.shape
    proj = hn @ w_in
    aq, ak, av, bq, bk, bv, cqk, cv, co, cg = jnp.split(proj, np.cumsum(IN_SPLITS)[:-1].tolist(), axis=-1)

    o_a = dilated_mixture(split_heads(aq, A_HEADS), split_heads(ak, A_HEADS), split_heads(av, A_HEADS),
                          alibi_slopes(A_HEADS))

    bq = bq.reshape(b, t, B_HEADS, 2, B_QK_DIM).transpose(3, 0, 2, 1, 4)
    bk = bk.reshape(b, t, B_HEADS, 2, B_QK_DIM).transpose(3, 0, 2, 1, 4)
    lam_init = 0.8 - 0.6 * math.exp(-0.3 * layer)
    lq1, lk1, lq2, lk2 = diff_lambda.astype(F32)
    lam = jnp.exp(jnp.sum(lq1 * lk1)) - jnp.exp(jnp.sum(lq2 * lk2)) + lam_init
    o_b = diff_attention(bq[0], bq[1], bk[0], bk[1], split_heads(bv, B_HEADS), lam, alibi_slopes(B_HEADS))

    cqk = jax.nn.silu(centred_conv(cqk, conv_w))
    cq, ck = jnp.split(cqk, 2, axis=-1)
    g = (cg.astype(F32) + gate_b.astype(F32)).reshape(b, t, 4, C_HEADS).transpose(2, 0, 3, 1)
    o_c = bidirectional_mlstm(split_heads(cq, C_HEADS), split_heads(ck, C_HEADS), split_heads(cv, C_HEADS),
                              g[0], jax.nn.log_sigmoid(g[1]), g[2], jax.nn.log_sigmoid(g[3]))
    o_c = o_c * jax.nn.sigmoid(split_heads(co, C_HEADS).astype(F32))

    heads = jnp.concatenate([o_a.astype(hn.dtype), o_b.astype(hn.dtype), o_c.astype(hn.dtype)], axis=1)
    heads = rmsnorm(heads, head_norm_g.reshape(N_MIX_HEADS, 1, HEAD_DIM))
    head_scale = np.ones(N_MIX_HEADS, np.float32)
    head_scale[A_HEADS:A_HEADS + B_HEADS] = 1.0 - lam_init
    heads = heads * jnp.asarray(head_scale, dtype=heads.dtype)[:, None, None]
    return heads.transpose(0, 2, 1, 3).reshape(b, t, MIX_W) @ w_out


def cross_attention(hn, mem_n, w_q, w_kv, w_o):
    b, t, _ = hn.shape
    m = mem_n.shape[1]
    q = (hn @ w_q).reshape(b, t, X_HEADS, X_HEAD_DIM)
    kv = (mem_n @ w_kv).reshape(b, m, 2, X_HEADS, X_HEAD_DIM)
    k, v = kv[:, :, 0], kv[:, :, 1]
    s = jnp.einsum('bthd,bmhd->bhtm', q, k).astype(F32) * (X_HEAD_DIM ** -0.5)
    a = jax.nn.softmax(s, axis=-1).astype(v.dtype)
    o = jnp.einsum('bhtm,bmhd->bthd', a, v).reshape(b, t, D_MODEL)
    return o @ w_o


def setup_inputs(seed: int = 0) -> dict:
    key = jax.random.key(seed)
    ks = jax.random.split(key, 18)

    def dense(k, shape, fan_in):
        return jax.random.normal(k, shape, F32) * (fan_in ** -0.5)

    def gain(k, shape):
        return 1.0 + 0.02 * jax.random.normal(k, shape, F32)

    gate_offset = jnp.asarray(np.array([0.0, FORGET_BIAS, 0.0, FORGET_BIAS], np.float32))[None, :, None]
    gate_b = (gate_offset + 0.1 * jax.random.normal(ks[5], (DEPTH, 4, C_HEADS), F32)).reshape(DEPTH, N_GATES)
    return {
        'x': jax.random.normal(ks[0], (BATCH, SEQ, D_MODEL), F32),
        'mem': jax.random.normal(ks[1], (BATCH, MEM_LEN, D_MODEL), F32),
        'norm_mix_g': gain(ks[2], (DEPTH, D_MODEL)),
        'w_in': dense(ks[3], (DEPTH, D_MODEL, IN_W), D_MODEL),
        'conv_w': dense(ks[4], (DEPTH, CONV_WIDTH, 2 * C_W), CONV_WIDTH),
        'gate_b': gate_b,
        'diff_lambda': 0.1 * jax.random.normal(ks[6], (DEPTH, 4, B_QK_DIM), F32),
        'head_norm_g': gain(ks[7], (DEPTH, MIX_W)),
        'w_out': dense(ks[8], (DEPTH, MIX_W, D_MODEL), MIX_W),
        'norm_x_g': gain(ks[9], (DEPTH, D_MODEL)),
        'norm_mem_g': gain(ks[10], (DEPTH, D_MODEL)),
        'w_xq': dense(ks[11], (DEPTH, D_MODEL, D_MODEL), D_MODEL),
        'w_xkv': dense(ks[12], (DEPTH, D_MODEL, 2 * D_MODEL), D_MODEL),
        'w_xo': dense(ks[13], (DEPTH, D_MODEL, D_MODEL), D_MODEL),
        'norm_mlp_g': gain(ks[14], (DEPTH, D_MODEL)),
        'w_up': dense(ks[15], (DEPTH, D_MODEL, D_FF), D_MODEL),
        'w_down': dense(ks[16], (DEPTH, D_FF, D_MODEL), D_FF),
        'final_norm_g': gain(ks[17], (D_MODEL,)),
    }


def reference(x, mem, norm_mix_g, w_in, conv_w, gate_b, diff_lambda, head_norm_g, w_out, norm_x_g, norm_mem_g,
              w_xq, w_xkv, w_xo, norm_mlp_g, w_up, w_down, final_norm_g):
    h = x
    for layer in range(DEPTH):
        h = h + hybrid_mixer(rmsnorm(h, norm_mix_g[layer]), layer, w_in[layer], conv_w[layer], gate_b[layer],
                             diff_lambda[layer], head_norm_g[layer], w_out[layer])
        h = h + cross_attention(rmsnorm(h, norm_x_g[layer]), rmsnorm(mem, norm_mem_g[layer]),
                                w_xq[layer], w_xkv[layer], w_xo[layer])
        u = rmsnorm(h, norm_mlp_g[layer]) @ w_up[layer]
        h = h + jnp.square(jax.nn.relu(u)) @ w_down[layer]
    return rmsnorm(h, final_norm_g)
```

```python
import math
from contextlib import ExitStack

import numpy as np
import ml_dtypes
import concourse.bass as bass
import concourse.mybir as mybir
from concourse.bass_utils import run_bass_kernel_spmd

F32 = mybir.dt.float32
BF16 = mybir.dt.bfloat16
ALU = mybir.AluOpType
AF = mybir.ActivationFunctionType
AX = mybir.AxisListType
NPBF = ml_dtypes.bfloat16

D = 2048
T = 4096
T2 = 2048
NB = 4
DEPTH = 2
EPS = 1e-6
DFF = 8192
MEM = 256
KC = D // 128


class Eng:
    def __init__(self, name, eng, sem):
        self.name = name
        self.eng = eng
        self.sem = sem
        self.cnt = 0
        self.known = {}
        self.snaps = [None]


class Region:
    __slots__ = ("writer", "readers")

    def __init__(self):
        self.writer = None
        self.readers = []


class Sched:
    NPOOL = 40

    def __init__(self, nc, stack):
        self.nc = nc
        self.ostack = stack
        self.stack = stack
        self.regions = {}
        self.sems = {}
        self.nwaits = 0
        self.nins = 0
        mk = lambda n: stack.enter_context(nc.semaphore(n))
        self.pe = Eng("pe", nc.tensor, mk("s_pe"))
        self.act = Eng("act", nc.scalar, mk("s_act"))
        self.dve = Eng("dve", nc.vector, mk("s_dve"))
        self.pool = Eng("pool", nc.gpsimd, mk("s_pool"))
        self.sp = Eng("sp", nc.sync, mk("s_sp"))
        self.engs = (self.pe, self.act, self.dve, self.pool, self.sp)
        self.by_sem = {e.sem.name: e for e in self.engs}
        self.dpool = [[mk(f"d_{i}"), 0] for i in range(self.NPOOL)]
        self.dfree = list(range(self.NPOOL))
        self._sbn = 0
        self.prefix = ""
        self.outkeys = []

    def begin_phase(self, name):
        self.prefix = name + "_"
        self.stack = ExitStack()

    def barrier(self):
        for E in self.engs:
            for B in self.engs:
                if B is not E and B.cnt > E.known.get(B.sem.name, 0):
                    E.eng.wait_ge(B.sem, B.cnt)
                    E.known[B.sem.name] = B.cnt
            for sem, cnt in self.dpool:
                if cnt > E.known.get(sem.name, 0):
                    E.eng.wait_ge(sem, cnt)
                    E.known[sem.name] = cnt

    def end_phase(self):
        self.barrier()
        self.stack.close()
        self.stack = self.ostack
        self.regions = {}
        for k, idx in self.sems.items():
            self.dfree.append(idx)
        self.sems = {}
        self.outkeys = []

    def sb(self, shape, dt, name=None):
        self._sbn += 1
        return self.stack.enter_context(self.nc.sbuf_tensor(self.prefix + (name or f"sb{self._sbn}"), list(shape), dt))

    def ps(self, shape, dt, name=None):
        self._sbn += 1
        return self.stack.enter_context(self.nc.psum_tensor(self.prefix + (name or f"ps{self._sbn}"), list(shape), dt))

    def dsem(self, semkey):
        if semkey not in self.sems:
            self.sems[semkey] = self.dfree.pop(0)
        return self.dpool[self.sems[semkey]]

    def reg(self, key):
        r = self.regions.get(key)
        if r is None:
            r = self.regions[key] = Region()
        return r

    def _need(self, E, reads, writes):
        need = {}

        def add(ev, raw):
            if ev is None:
                return
            sem, val = ev
            if (not raw) and sem is E.sem:
                return
            k = sem.name
            if need.get(k, (None, 0))[1] < val:
                need[k] = (sem, val)

        for r in reads:
            add(self.reg(r).writer, True)
        for w in writes:
            rg = self.reg(w)
            add(rg.writer, False)
            for ev in rg.readers:
                add(ev, False)
        return need

    def _do_waits(self, E, need):
        for k, (sem, val) in need.items():
            if E.known.get(k, 0) >= val:
                continue
            E.eng.wait_ge(sem, val)
            self.nwaits += 1
            E.known[k] = val
            src = self.by_sem.get(k)
            if src is not None and src is not E:
                snap = src.snaps[val] if val < len(src.snaps) else None
                if snap:
                    for kk, vv in snap.items():
                        if E.known.get(kk, 0) < vv:
                            E.known[kk] = vv

    def _record(self, ev, reads, writes):
        for r in reads:
            rg = self.reg(r)
            rg.readers.append(ev)
            if len(rg.readers) > 64:
                best = {}
                for s, v in rg.readers:
                    if best.get(s.name, (None, 0))[1] < v:
                        best[s.name] = (s, v)
                rg.readers = list(best.values())
        for w in writes:
            rg = self.reg(w)
            rg.writer = ev
            rg.readers = []

    def op(self, E, fn, reads=(), writes=()):
        self._do_waits(E, self._need(E, reads, writes))
        ins = fn(E.eng)
        E.cnt += 1
        ins.then_inc(E.sem, 1)
        self.nins += 1
        E.snaps.append(dict(E.known))
        self._record((E.sem, E.cnt), reads, writes)
        return ins

    def dma(self, E, out, in_, reads=(), writes=(), semkey="dma", **kw):
        self._do_waits(E, self._need(E, reads, writes))
        ent = self.dsem(semkey)
        ins = E.eng.dma_start(out=out, in_=in_, **kw)
        ent[1] += 16
        ins.then_inc(ent[0], 16)
        self.nins += 1
        self._record((ent[0], ent[1]), reads, writes)
        return ins

    def collective(self, kind, src, dst, groups):
        ent = self.dsem("cc")
        ins = self.pool.eng.collective_compute(kind, ALU.bypass, replica_groups=groups, ins=[src], outs=[dst])
        ent[1] += 1
        ins.then_inc(ent[0], 1)
        self.pool.eng.wait_ge(ent[0], ent[1])
        self.pool.known[ent[0].name] = ent[1]

    def out_dma(self, E, out, in_, reads, semkey):
        key = f"out{len(self.outkeys)}"
        self.outkeys.append(key)
        return self.dma(E, out, in_, reads=reads, writes=[key], semkey=semkey)

    def finish(self, keys=None):
        self.barrier()


class Rot:
    def __init__(self, S, n, shape, dt, name, psum=False):
        self.tiles = [(S.ps if psum else S.sb)(shape, dt, f"{name}{i}") for i in range(n)]
        self.keys = [f"{name}{i}" for i in range(n)]
        self.i = 0

    def next(self):
        t, k = self.tiles[self.i], self.keys[self.i]
        self.i = (self.i + 1) % len(self.tiles)
        return t, k


def load_weight_block(S, wrot, W, c0, ncols, kc, tagkeys):
    wt, wk = wrot.next()
    Wv = W.rearrange("(k p) n -> p k n", p=128)
    step = max(1, 512 * 4 // ncols)
    step = min(step, kc)
    for k0 in range(0, kc, step):
        S.dma(S.pool, wt[:, k0:k0 + step, 0:ncols], Wv[:, k0:k0 + step, c0:c0 + ncols],
              reads=tagkeys, writes=[wk], semkey=wk)
    return wt, wk


def rms_stats_fm(S, srcT, srckeys, kc, ntok, ones_bf, psrot, sqrot, rstd, rstdkey, dim, t0=0):
    pt, pk = psrot.next()
    for j in range(kc):
        sq, sqk = sqrot.next()
        S.op(S.act, lambda e: e.activation(sq[:, 0:ntok], srcT[:, j, t0:t0 + ntok], AF.Square),
             reads=[srckeys[j]], writes=[sqk])
        S.op(S.pe, lambda e: e.matmul(pt[:, 0:ntok], ones_bf[:, :], sq[:, 0:ntok], start=(j == 0), stop=(j == kc - 1)),
             reads=[sqk, "ones"], writes=[pk])
    S.op(S.act, lambda e: e.activation(rstd[:, t0:t0 + ntok], pt[:, 0:ntok], AF.Ln, bias=S.eps_col[:, 0:1], scale=1.0 / dim),
         reads=[pk, "consts"], writes=[rstdkey])
    S.op(S.act, lambda e: e.activation(rstd[:, t0:t0 + ntok], rstd[:, t0:t0 + ntok], AF.Exp, scale=-0.5),
         reads=[rstdkey], writes=[rstdkey])


def setup_consts(S):
    S.ones_bf = S.sb([128, 128], BF16, "ones_bf")
    S.eps_col = S.sb([128, 1], F32, "eps_col")
    S.op(S.pool, lambda e: e.memset(S.ones_bf[:, :], 1.0), writes=["ones"])
    S.op(S.pool, lambda e: e.memset(S.eps_col[:, :], EPS), writes=["consts"])


NFM_BF = 2560
NFM_F = 2304
NTM = 2048
NG = 24


def phase_A(S, hT, gcol, wfm, wtm, ofm_bf, ofm_f, otm_bf, otm_g):
    nc = S.nc
    if True:
        setup_consts(S)
        g_sb = S.sb([128, KC], F32, "g_sb")
        S.dma(S.sp, g_sb[:, :], gcol, writes=["g"], semkey="g")
        hnT = S.sb([128, KC, T2], BF16, "hnT")
        hblk = S.sb([128, KC, 512], F32, "hblk")
        rstd = S.sb([128, 512], F32, "rstd")
        psrot = Rot(S, 8, [128, 512], F32, "ps", psum=True)
        sqrot = Rot(S, 3, [128, 512], BF16, "sq")
        hTv = hT.rearrange("(k p) t -> p k t", p=128)
        for tb in range(4):
            for k0 in range(0, KC, 4):
                S.dma(S.sp, hblk[:, k0:k0 + 4, :], hTv[:, k0:k0 + 4, tb * 512:(tb + 1) * 512],
                      writes=[f"hblk{j}" for j in range(k0, k0 + 4)], semkey=f"hblk{k0}")
            rms_stats_fm(S, hblk, [f"hblk{j}" for j in range(KC)], KC, 512, S.ones_bf, psrot, sqrot, rstd, "rstd", D)
            for j in range(KC):
                S.op(S.dve, lambda e, j=j: e.scalar_tensor_tensor(
                    out=hnT[:, j, tb * 512:(tb + 1) * 512], in0=hblk[:, j, :], scalar=g_sb[:, j:j + 1],
                    in1=rstd[:, :], op0=ALU.mult, op1=ALU.mult),
                    reads=[f"hblk{j}", "g", "rstd"], writes=[f"hnT{tb}"])
        hn_keys = [f"hnT{tb}" for tb in range(4)]
        wrot = Rot(S, 2, [128, KC, 512], BF16, "wbuf")
        stg_bf = Rot(S, 2, [128, T2], BF16, "stgb")
        stg_f = Rot(S, 2, [128, T2], F32, "stgf")
        nfm = NFM_BF + NFM_F
        evi = 0
        for c0 in range(0, nfm, 512):
            ncb = min(512, nfm - c0)
            wt, wk = load_weight_block(S, wrot, wfm, c0, ncb, KC, [])
            for cc in range(ncb // 128):
                col = c0 + cc * 128
                isbf = col < NFM_BF
                stg, sk = (stg_bf if isbf else stg_f).next()
                for tb in range(4):
                    pt, pk = psrot.next()

                    def mm(e, pt=pt, cc=cc, tb=tb, wt=wt):
                        for k in range(KC):
                            ins = e.matmul(pt[:, :], wt[:, k, cc * 128:(cc + 1) * 128], hnT[:, k, tb * 512:(tb + 1) * 512],
                                           start=(k == 0), stop=(k == KC - 1))
                        return ins
                    S.op(S.pe, mm, reads=[wk, hn_keys[tb]], writes=[pk])
                    E = S.act if evi % 2 == 0 else S.dve
                    evi += 1
                    if E is S.act:
                        S.op(E, lambda e, pt=pt, stg=stg, tb=tb: e.copy(stg[:, tb * 512:(tb + 1) * 512], pt[:, :]),
                             reads=[pk], writes=[sk])
                    else:
                        S.op(E, lambda e, pt=pt, stg=stg, tb=tb: e.tensor_copy(stg[:, tb * 512:(tb + 1) * 512], pt[:, :]),
                             reads=[pk], writes=[sk])
                if isbf:
                    S.out_dma(S.sp, ofm_bf[col:col + 128, :], stg[:, :], [sk], sk)
                else:
                    S.out_dma(S.sp, ofm_f[col - NFM_BF:col - NFM_BF + 128, :], stg[:, :], [sk], sk)
        stg_t = Rot(S, 2, [128, 512], BF16, "stgt")
        stg_g = Rot(S, 2, [128, NG], F32, "stgg")
        for c0 in range(0, NTM + NG, 512):
            ncols = min(512, NTM + NG - c0)
            wt, wk = load_weight_block(S, wrot, wtm, c0, ncols, KC, [])
            for tt in range(T2 // 128):
                pt, pk = psrot.next()

                def mm(e, pt=pt, tt=tt, wt=wt, ncols=ncols):
                    for k in range(KC):
                        ins = e.matmul(pt[:, 0:ncols], hnT[:, k, tt * 128:(tt + 1) * 128], wt[:, k, 0:ncols],
                                       start=(k == 0), stop=(k == KC - 1))
                    return ins
                S.op(S.pe, mm, reads=[wk, hn_keys[tt // 4]], writes=[pk])
                if ncols == 512:
                    stg, sk = stg_t.next()
                    S.op(S.act if tt % 2 == 0 else S.dve,
                         (lambda e, pt=pt, stg=stg: e.copy(stg[:, :], pt[:, :])) if tt % 2 == 0 else
                         (lambda e, pt=pt, stg=stg: e.tensor_copy(stg[:, :], pt[:, :])),
                         reads=[pk], writes=[sk])
                    S.out_dma(S.sp, otm_bf[c0 // 1024, tt * 128:(tt + 1) * 128, (c0 % 1024):(c0 % 1024) + 512], stg[:, :], [sk], sk)
                else:
                    stg, sk = stg_g.next()
                    S.op(S.dve, lambda e, pt=pt, stg=stg: e.tensor_copy(stg[:, :], pt[:, 0:NG]), reads=[pk], writes=[sk])
                    S.out_dma(S.sp, otm_g[0, tt * 128:(tt + 1) * 128, :], stg[:, 0:12], [sk], sk)
                    S.out_dma(S.sp, otm_g[1, tt * 128:(tt + 1) * 128, :], stg[:, 12:24], [sk], sk)


WA_LEN = 2944
WB_LEN = 8064
NEGM = -30000.0


def head_norm_store(S, O, okey, ntok, gcol_ap, psrot, sqrot, outT_rows, t0, stgrot, tmp_rot):
    pt, pk = psrot.next()
    sq, sqk = sqrot.next()
    S.op(S.act, lambda e: e.activation(sq[:, 0:ntok], O[:, 0:ntok], AF.Square), reads=[okey], writes=[sqk])
    S.op(S.pe, lambda e: e.matmul(pt[:, 0:ntok], S.ones_bf[:, :], sq[:, 0:ntok], start=True, stop=True),
         reads=[sqk, "ones"], writes=[pk])
    r, rk = tmp_rot.next()
    S.op(S.act, lambda e: e.activation(r[:, 0:ntok], pt[:, 0:ntok], AF.Ln, bias=S.eps_col[:, 0:1], scale=1.0 / 128),
         reads=[pk, "consts"], writes=[rk])
    S.op(S.act, lambda e: e.activation(r[:, 0:ntok], r[:, 0:ntok], AF.Exp, scale=-0.5), reads=[rk], writes=[rk])
    stg, sk = stgrot.next()
    S.op(S.dve, lambda e: e.scalar_tensor_tensor(out=stg[:, 0:ntok], in0=O[:, 0:ntok], scalar=gcol_ap,
                                                 in1=r[:, 0:ntok], op0=ALU.mult, op1=ALU.mult),
         reads=[okey, rk, "hng"], writes=[sk])
    S.out_dma(S.sp, outT_rows[:, t0 // T2, (t0 % T2):(t0 % T2) + ntok], stg[:, 0:ntok], [sk], sk)


def phase_B(S, layer, bfin, ffin, tmin, cg, wA, wB, convw, gb, dl, hng, cf, outT):
    nc = S.nc
    lam_init = 0.8 - 0.6 * math.exp(-0.3 * layer)
    aqT = bfin[0:384, :].rearrange("(h p) t -> h p t", p=128)
    akT = bfin[384:768, :].rearrange("(h p) t -> h p t", p=128)
    bqT = bfin[768:1024, :].rearrange("(h p) t -> h p t", p=128)
    bkT = bfin[1024:1280, :].rearrange("(h p) t -> h p t", p=128)
    cqT = ffin[0:384, :].rearrange("(h p) t -> h p t", p=128)
    ckT = ffin[384:768, :].rearrange("(h p) t -> h p t", p=128)
    coT = ffin[768:1152, :].rearrange("(h p) t -> h p t", p=128)
    av, bv, cv = tmin[:, 0:384], tmin[:, 384:640], tmin[:, 640:1024]
    NCH = T // 128
    if True:
        setup_consts(S)
        cf_sb = S.sb([128, 8, 128], F32, "cf_sb")
        S.dma(S.sp, cf_sb[:, :, :], cf, writes=["cf"], semkey="cf")
        Ltri, Utri, MnF, MnB, identf = (cf_sb[:, i, :] for i in range(5))
        identb = S.sb([128, 128], BF16, "identb")
        S.op(S.dve, lambda e: e.tensor_copy(identb[:, :], identf), reads=["cf"], writes=["identb"])
        ones_f = S.sb([128, 128], F32, "ones_f")
        S.op(S.pool, lambda e: e.memset(ones_f[:, :], 1.0), writes=["ones_f"])
        convw_sb = S.sb([128, 6, 3], F32, "convw_sb")
        S.dma(S.sp, convw_sb[:, :, :], convw, writes=["convw"], semkey="sm1")
        gb_sb = S.sb([128, 12], F32, "gb_sb")
        S.dma(S.sp, gb_sb[:, :], gb, writes=["gb"], semkey="sm2")
        dl_sb = S.sb([128, 256], F32, "dl_sb")
        S.dma(S.sp, dl_sb[:, :], dl, writes=["dl"], semkey="sm3")
        hng_sb = S.sb([128, 8], F32, "hng_sb")
        S.dma(S.sp, hng_sb[:, :], hng, writes=["hng0"], semkey="sm4")
        lamt = S.sb([128, 8], F32, "lamt")
        prod = S.sb([128, 128], F32, "prod")
        S.op(S.dve, lambda e: e.tensor_tensor(prod[:, 0:64], dl_sb[:, 0:64], dl_sb[:, 64:128], ALU.mult), reads=["dl"], writes=["prod"])
        S.op(S.dve, lambda e: e.tensor_tensor(prod[:, 64:128], dl_sb[:, 128:192], dl_sb[:, 192:256], ALU.mult), reads=["dl"], writes=["prod"])
        S.op(S.dve, lambda e: e.reduce_sum(lamt[:, 0:1], prod[:, 0:64], axis=AX.X), reads=["prod"], writes=["lam0"])
        S.op(S.dve, lambda e: e.reduce_sum(lamt[:, 1:2], prod[:, 64:128], axis=AX.X), reads=["prod"], writes=["lam1"])
        S.op(S.act, lambda e: e.activation(lamt[:, 2:4], lamt[:, 0:2], AF.Exp), reads=["lam0", "lam1"], writes=["lam2"])
        S.op(S.dve, lambda e: e.scalar_tensor_tensor(out=lamt[:, 4:5], in0=lamt[:, 3:4], scalar=-lam_init, in1=lamt[:, 2:3],
                                                     op0=ALU.add, op1=ALU.subtract), reads=["lam2"], writes=["neglam"])
        S.op(S.dve, lambda e: e.tensor_scalar(hng_sb[:, 3:5], hng_sb[:, 3:5], 1.0 - lam_init, None, ALU.mult),
             reads=["hng0"], writes=["hng"])
        S.reg("hng").readers = []

        qrot = Rot(S, 2, [128, T], BF16, "qT")
        krot = Rot(S, 2, [128, T], BF16, "kT")
        vrot = Rot(S, 2, [128, NCH, 129], BF16, "vv")
        wrot = Rot(S, 2, [128, WB_LEN], BF16, "wst")
        ps_st = Rot(S, 4, [128, 512], F32, "pst", psum=True)
        ps_acc = Rot(S, 4, [128, 512], F32, "pac", psum=True)
        erot = Rot(S, 4, [128, 512], BF16, "ee")
        prot = Rot(S, 7, [128, 512], BF16, "pp")
        sqrot = Rot(S, 2, [128, 512], BF16, "sq")
        t32 = Rot(S, 6, [128, 512], F32, "t32")
        orot = Rot(S, 2, [128, 512], F32, "oo")
        stgrot = Rot(S, 2, [128, 512], BF16, "stg")
        for vt in vrot.tiles:
            pass
        S.op(S.pool, lambda e: e.memset(vrot.tiles[0][:, :, 128:129], 1.0), writes=["vv0"])
        S.op(S.pool, lambda e: e.memset(vrot.tiles[1][:, :, 128:129], 1.0), writes=["vv1"])

        def load_head(qsrc, ksrc, vsrc, vcol0, wsrc, wlen):
            qt, qk = qrot.next()
            kt_, kk = krot.next()
            vt, vk = vrot.next()
            if qsrc is not None:
                for i in range(4):
                    S.dma(S.sp, qt[:, i * 1024:(i + 1) * 1024], qsrc[:, i * 1024:(i + 1) * 1024], writes=[qk], semkey=qk)
                    S.dma(S.sp, kt_[:, i * 1024:(i + 1) * 1024], ksrc[:, i * 1024:(i + 1) * 1024], writes=[kk], semkey=kk)
            vsv = vsrc.rearrange("(c p) n -> p c n", p=128)
            for i in range(4):
                S.dma(S.sp, vt[:, i * 8:(i + 1) * 8, 0:128], vsv[:, i * 8:(i + 1) * 8, vcol0:vcol0 + 128], writes=[vk], semkey=vk)
            wt, wk = None, None
            if wsrc is not None:
                wt, wk = wrot.next()
                for i in range(0, wlen, 2048):
                    n = min(2048, wlen - i)
                    S.dma(S.sp, wt[:, i:i + n], wsrc[:, i:i + n], writes=[wk], semkey=wk)
            return (qt, qk), (kt_, kk), (vt, vk), (wt, wk)

        mul_n = [0]

        def attn_s1(it):
            (qt, qk), (ktile, kk), (vt, vk), (wt, wk) = it["Q"], it["K"], it["V"], it["W"]
            p0, p1 = it["kpart"]
            kt, qc, i0 = it["kt"], it["qc"], it["i0"]
            stp, stk = ps_st.next()
            S.op(S.pe, lambda e: e.matmul(stp[:, :], ktile[p0:p1, kt * 128:(kt + 1) * 128], qt[p0:p1, qc * 512:(qc + 1) * 512],
                                          start=True, stop=True), reads=[qk, kk], writes=[stk])
            et, ek = erot.next()
            S.op(S.act, lambda e: e.activation(et[:, :], stp[:, :], AF.Exp, scale=it["scale"]), reads=[stk], writes=[ek])
            pt_, pk_ = prot.next()
            ME = S.pool if (mul_n[0] % 2 == 0) else S.dve
            mul_n[0] += 1
            S.op(ME, lambda e: e.tensor_tensor(pt_[:, :], et[:, :], wt[:, i0:i0 + 512], ALU.mult), reads=[ek, wk], writes=[pk_])
            return pt_, pk_

        def attn_s2(it, st):
            pt_, pk_ = st
            (vt, vk) = it["V"]
            kt = it["kt"]
            (ao, aok), (ad, adk) = it["acc_o"], it["acc_d"]
            S.op(S.pe, lambda e: e.matmul(ao[:, :], vt[:, kt, 0:128], pt_[:, :], start=it["first"], stop=it["last"]),
                 reads=[vk, pk_], writes=[aok])
            S.op(S.pe, lambda e: e.matmul(ad[:, :], S.ones_bf[:, :], pt_[:, :], start=it["first"], stop=it["last"]),
                 reads=["ones", pk_], writes=[adk])
            if it.get("epi") is not None:
                it["epi"]()

        def gen_blocks():
            for h in range(3):
                Q, Kt, V, W = load_head(aqT[h], akT[h], av, h * 128, wA[h], WA_LEN)
                for qc in range(8):
                    kts = [k for k in range(4 * qc - 8, 4 * qc + 12) if 0 <= k < NCH]
                    acc_o, acc_d = ps_acc.next(), ps_acc.next()

                    def epiA(h=h, qc=qc, acc_o=acc_o, acc_d=acc_d):
                        rd, rdk = t32.next()
                        S.op(S.dve, lambda e: e.reciprocal(rd[:, :], acc_d[0][:, :]), reads=[acc_d[1]], writes=[rdk])
                        O, ok = orot.next()
                        S.op(S.dve, lambda e: e.tensor_tensor(O[:, :], acc_o[0][:, :], rd[:, :], ALU.mult),
                             reads=[acc_o[1], rdk], writes=[ok])
                        head_norm_store(S, O, ok, 512, hng_sb[:, h:h + 1], ps_st, sqrot,
                                        outT.rearrange("two r t -> r two t")[h * 128:(h + 1) * 128], qc * 512, stgrot, t32)
                    for n, kt in enumerate(kts):
                        yield dict(kt=kt, qc=qc, kpart=(0, 128), Q=Q, K=Kt, V=V, W=W, i0=qc * 512 - kt * 128 + 1408,
                                   scale=128 ** -0.5, acc_o=acc_o, acc_d=acc_d, first=(n == 0), last=(n == len(kts) - 1),
                                   epi=epiA if n == len(kts) - 1 else None)
            for h in range(2):
                Q, Kt, V, W = load_head(bqT[h], bkT[h], bv, h * 128, wB[h], WB_LEN)
                for qc in range(8):
                    accs = [ps_acc.next() for _ in range(4)]

                    def epiB(h=h, qc=qc, accs=accs):
                        r1, r1k = t32.next()
                        r2, r2k = t32.next()
                        S.op(S.dve, lambda e: e.reciprocal(r1[:, :], accs[1][0][:, :]), reads=[accs[1][1]], writes=[r1k])
                        S.op(S.dve, lambda e: e.reciprocal(r2[:, :], accs[3][0][:, :]), reads=[accs[3][1]], writes=[r2k])
                        S.op(S.dve, lambda e: e.tensor_tensor(r1[:, :], accs[0][0][:, :], r1[:, :], ALU.mult),
                             reads=[accs[0][1], r1k], writes=[r1k])
                        S.op(S.dve, lambda e: e.tensor_tensor(r2[:, :], accs[2][0][:, :], r2[:, :], ALU.mult),
                             reads=[accs[2][1], r2k], writes=[r2k])
                        O, ok = orot.next()
                        S.op(S.dve, lambda e: e.scalar_tensor_tensor(out=O[:, :], in0=r2[:, :], scalar=lamt[:, 4:5], in1=r1[:, :],
                                                                     op0=ALU.mult, op1=ALU.add), reads=[r1k, r2k, "neglam"], writes=[ok])
                        head_norm_store(S, O, ok, 512, hng_sb[:, 3 + h:4 + h], ps_st, sqrot,
                                        outT.rearrange("two r t -> r two t")[(3 + h) * 128:(4 + h) * 128], qc * 512, stgrot, t32)
                    for i in range(2):
                        for kt in range(NCH):
                            yield dict(kt=kt, qc=qc, kpart=(64 * i, 64 * i + 64), Q=Q, K=Kt, V=V, W=W,
                                       i0=qc * 512 - kt * 128 + 3968, scale=64 ** -0.5,
                                       acc_o=accs[2 * i], acc_d=accs[2 * i + 1], first=(kt == 0), last=(kt == NCH - 1),
                                       epi=epiB if (i == 1 and kt == NCH - 1) else None)

        LA = 3
        pend = []
        for it in gen_blocks():
            pend.append((it, attn_s1(it)))
            if len(pend) > LA:
                attn_s2(*pend.pop(0))
        while pend:
            attn_s2(*pend.pop(0))

        build_mlstm(S, nc, dict(locals()))


def build_mlstm(S, nc, env):
    g = lambda n: env[n]
    cqT, ckT, coT, cv, cg = g("cqT"), g("ckT"), g("coT"), g("cv"), g("cg")
    convw_sb, gb_sb, hng_sb = g("convw_sb"), g("gb_sb"), g("hng_sb")
    qrot, krot, vrot, ps_st, ps_acc, t32, orot, stgrot, sqrot = (g(n) for n in
        ("qrot", "krot", "vrot", "ps_st", "ps_acc", "t32", "orot", "stgrot", "sqrot"))
    erot, prot = g("erot"), g("prot")
    outT, Ltri, Utri, MnF, MnB, identf, identb, load_head, NCH = (g(n) for n in
        ("outT", "Ltri", "Utri", "MnF", "MnB", "identf", "identb", "load_head", "NCH"))
    qs = 128 ** -0.5
    GT = S.sb([128, NCH, 12], F32, "GT")
    S.dma(S.sp, GT[:, :, :], cg.rearrange("(c p) g -> p c g", p=128), writes=["GT"], semkey="GT")
    LI = S.sb([128, 6, NCH], F32, "LI")
    XF = S.sb([128, 6, NCH], F32, "XF")
    LF = S.sb([128, 6, NCH], F32, "LF")
    TA = S.sb([128, 6, NCH], F32, "TA")
    NEGA = S.sb([128, 6, NCH], F32, "NEGA")
    for d in range(2):
        for h in range(3):
            l = d * 3 + h
            ci, cf_ = d * 6 + h, d * 6 + 3 + h
            S.op(S.dve, lambda e: e.tensor_scalar(LI[:, l, :], GT[:, :, ci], gb_sb[:, ci:ci + 1], None, ALU.add),
                 reads=["GT", "gb"], writes=["LI"])
            S.op(S.dve, lambda e: e.tensor_scalar(XF[:, l, :], GT[:, :, cf_], gb_sb[:, cf_:cf_ + 1], None, ALU.add),
                 reads=["GT", "gb"], writes=["XF"])
    S.op(S.dve, lambda e: e.tensor_scalar(TA[:, :, :], XF[:, :, :], -1.0, None, ALU.mult), reads=["XF"], writes=["TA"])
    S.op(S.dve, lambda e: e.tensor_tensor(TA[:, :, :], TA[:, :, :], XF[:, :, :], ALU.max), reads=["XF", "TA"], writes=["TA"])
    S.op(S.act, lambda e: e.activation(TA[:, :, :], TA[:, :, :], AF.Exp, scale=-1.0), reads=["TA"], writes=["TA"])
    S.op(S.act, lambda e: e.activation(TA[:, :, :], TA[:, :, :], AF.Ln, bias=1.0, scale=1.0), reads=["TA"], writes=["TA"])
    S.op(S.dve, lambda e: e.tensor_scalar(LF[:, :, :], XF[:, :, :], 0.0, None, ALU.min), reads=["XF"], writes=["LF"])
    S.op(S.dve, lambda e: e.tensor_tensor(LF[:, :, :], LF[:, :, :], TA[:, :, :], ALU.subtract), reads=["LF", "TA"], writes=["LF"])
    for d in range(2):
        Tri = Ltri if d == 0 else Utri
        bp, bpk = ps_st.next()
        S.op(S.pe, lambda e: e.matmul(bp[:, 0:3 * NCH], Tri, LF[:, 3 * d:3 * d + 3, :], start=True, stop=True),
             reads=["cf", "LF"], writes=[bpk])
        S.op(S.dve, lambda e: e.tensor_tensor(NEGA[:, 3 * d:3 * d + 3, :], LI[:, 3 * d:3 * d + 3, :],
                                              bp[:, 0:3 * NCH].rearrange("p (a b) -> p a b", a=3), ALU.subtract),
             reads=["LI", bpk], writes=["NEGA"])
    F0 = S.sb([128, T], F32, "F0")
    F1 = S.sb([128, T], F32, "F1")
    HS = S.sb([128, T], F32, "HS")
    Ktm = S.sb([128, NCH, 128], BF16, "Ktm")
    Cring = S.sb([128, 4, 2, 129], F32, "Cring")
    Cbf2 = S.sb([128, 2, 2, 128], BF16, "Cbf2")
    nbc2 = S.sb([128, 2, 2, 128], BF16, "nbc2")
    cf_sb = g("cf_sb")
    dtrot = Rot(S, 3, [128, 256], F32, "dtt")
    dmrot = Rot(S, 4, [128, 256], F32, "dtm")
    iwrot = Rot(S, 5, [128, 256], F32, "iww")
    scrot = Rot(S, 5, [128, 256], BF16, "sct")
    qsrot = Rot(S, 5, [128, 256], BF16, "qst")
    kgrot = Rot(S, 6, [128, 128], BF16, "kgg")
    rdrot = Rot(S, 3, [128, 256], F32, "rdd")

    def conv_silu(src, ci, dst, dkey):
        for i in range(4):
            S.dma(S.sp, F0[:, i * 1024:(i + 1) * 1024], src[:, i * 1024:(i + 1) * 1024], writes=["F0"], semkey="F0")
        S.op(S.dve, lambda e: e.tensor_scalar(F1[:, :], F0[:, :], convw_sb[:, ci, 1:2], None, ALU.mult),
             reads=["F0", "convw"], writes=["F1"])
        S.op(S.dve, lambda e: e.scalar_tensor_tensor(out=F1[:, 1:T], in0=F0[:, 0:T - 1], scalar=convw_sb[:, ci, 0:1],
                                                     in1=F1[:, 1:T], op0=ALU.mult, op1=ALU.add),
             reads=["F0", "F1", "convw"], writes=["F1"])
        S.op(S.dve, lambda e: e.scalar_tensor_tensor(out=F1[:, 0:T - 1], in0=F0[:, 1:T], scalar=convw_sb[:, ci, 2:3],
                                                     in1=F1[:, 0:T - 1], op0=ALU.mult, op1=ALU.add),
             reads=["F0", "F1", "convw"], writes=["F1"])
        S.op(S.act, lambda e: e.activation(dst[:, :], F1[:, :], AF.Silu), reads=["F1"], writes=[dkey])

    for h in range(3):
        (qt, qk), (kt_, kk), (vt, vk), _ = load_head(None, None, cv, h * 128, None, 0)
        conv_silu(cqT[h], h, qt, qk)
        conv_silu(ckT[h], 3 + h, kt_, kk)
        for c0 in range(0, NCH, 4):
            tp, tpk = ps_st.next()

            def tr(e):
                for j in range(4):
                    ins = e.matmul(tp[:, j * 128:(j + 1) * 128], kt_[:, (c0 + j) * 128:(c0 + j + 1) * 128], identb[:, :],
                                   start=True, stop=True)
                return ins
            S.op(S.pe, tr, reads=[kk, "identb"], writes=[tpk])
            if (c0 // 4) % 2 == 0:
                S.op(S.act, lambda e: e.copy(Ktm[:, c0:c0 + 4, :], tp[:, :].rearrange("p (a b) -> p a b", a=4)),
                     reads=[tpk], writes=["Ktm"])
            else:
                S.op(S.dve, lambda e: e.tensor_copy(Ktm[:, c0:c0 + 4, :], tp[:, :].rearrange("p (a b) -> p a b", a=4)),
                     reads=[tpk], writes=["Ktm"])
        for i in range(4):
            S.dma(S.sp, F0[:, i * 1024:(i + 1) * 1024], coT[h][:, i * 1024:(i + 1) * 1024], writes=["F0"], semkey="F0")
        S.op(S.act, lambda e: e.activation(F0[:, :], F0[:, :], AF.Sigmoid), reads=["F0"], writes=["F0"])
        S.op(S.pool, lambda e: e.memset(Cring[:, 0, :, :], 0.0), writes=["C0"])
        def halves_of(i):
            cf_, cb_ = i, NCH - 1 - i
            lo, hi = min(cf_, cb_), max(cf_, cb_)
            dirs = (0, 1) if cf_ < cb_ else (1, 0)
            return lo, hi, dirs, ((lo, dirs[0]), (hi, dirs[1]))

        def stage_P1(i):
            lo, hi, dirs, halves = halves_of(i)
            first, lastc = (i == 0), (i == NCH - 1)
            pb, pbk = ps_st.next()

            def mmE(e):
                for x, (c, d) in enumerate(halves):
                    lfb = LF[:, d * 3 + h, c:c + 1].to_broadcast([128, 128])
                    e.matmul(pb[:, x * 128:(x + 1) * 128], lfb, Ltri if d == 0 else Utri, start=True, stop=True)
                for x, c in enumerate((lo, hi)):
                    ins = e.matmul(pb[:, 256 + x * 128:256 + (x + 1) * 128], kt_[:, c * 128:(c + 1) * 128], qt[:, c * 128:(c + 1) * 128],
                                   start=True, stop=True)
                return ins
            S.op(S.pe, mmE, reads=["LF", "cf", kk, qk], writes=[pbk])
            dtt, dtk = dtrot.next()
            iwt, iwk = iwrot.next()
            for x, (c, d) in enumerate(halves):
                S.op(S.act, lambda e: e.activation(dtt[:, x * 128:(x + 1) * 128], pb[:, x * 128:(x + 1) * 128], AF.Exp,
                                                   bias=NEGA[:, d * 3 + h, c:c + 1]), reads=[pbk, "NEGA"], writes=[dtk])
            S.op(S.act, lambda e: e.activation(iwt[:, :], pb[:, 0:256], AF.Exp), reads=[pbk], writes=[iwk])
            dtm, dmk = dmrot.next()
            mk0 = 5 if dirs[0] == 0 else 6
            S.op(S.pool, lambda e: e.tensor_tensor(dtm[:, :], dtt[:, :], cf_sb[:, mk0:mk0 + 2, :].rearrange("p a b -> p (a b)"), ALU.mult),
                 reads=[dtk, "cf"], writes=[dmk])
            sct, sck = scrot.next()
            S.op(S.dve, lambda e: e.tensor_tensor(sct[:, :], pb[:, 256:512], dtm[:, :], ALU.mult), reads=[pbk, dmk], writes=[sck])
            qst, qsk = None, None
            if not first:
                qst, qsk = qsrot.next()
                S.op(S.pool, lambda e: e.tensor_tensor(qst[:, :].rearrange("p (a b) -> p a b", a=2),
                                                       qt[:, :].rearrange("p (c b) -> p c b", b=128)[:, lo:hi + 1:hi - lo, :],
                                                       iwt[:, :].rearrange("p (a b) -> p a b", a=2), ALU.mult),
                     reads=[qk, iwk], writes=[qsk])
            kgs = None
            if not lastc:
                kgs = []
                for x, (c, d) in enumerate(halves):
                    tl = 127 if d == 0 else 0
                    kg, kgk = kgrot.next()
                    S.op(S.act, lambda e: e.mul(kg[:, :], Ktm[:, c, :], dtm[:, x * 128 + tl:x * 128 + tl + 1]),
                         reads=["Ktm", dmk], writes=[kgk])
                    kgs.append((kg, kgk))
            return dict(i=i, first=first, lastc=lastc, iwt=iwt, iwk=iwk, sct=sct, sck=sck, qst=qst, qsk=qsk, kgs=kgs)

        def stage_P2(u):
            i = u["i"]
            if u["lastc"]:
                return
            lo, hi, dirs, halves = halves_of(i)
            dcb, dck = ps_acc.next()

            def mmdc(e):
                for x, (c, d) in enumerate(halves):
                    ins = e.matmul(dcb[:, x * 129:(x + 1) * 129], u["kgs"][x][0][:, :], vt[:, c, 0:129], start=True, stop=True)
                return ins
            S.op(S.pe, mmdc, reads=[u["kgs"][0][1], u["kgs"][1][1], vk], writes=[dck])
            src, dst = i % 4, (i + 1) % 4
            for x, (c, d) in enumerate(halves):
                tl = 127 if d == 0 else 0
                S.op(S.dve, lambda e: e.scalar_tensor_tensor(out=Cring[:, dst, d, :], in0=Cring[:, src, d, :],
                                                             scalar=u["iwt"][:, x * 128 + tl:x * 128 + tl + 1],
                                                             in1=dcb[:, x * 129:(x + 1) * 129], op0=ALU.mult, op1=ALU.add),
                     reads=[f"C{src}", u["iwk"], dck], writes=[f"C{dst}"])

        def stage_Qc(u):
            i = u["i"]
            if u["first"]:
                return
            sl, par = i % 4, i % 2
            S.op(S.act, lambda e: e.copy(Cbf2[:, par, :, :], Cring[:, sl, :, 0:128]), reads=[f"C{sl}"], writes=[f"Cbf{par}"])
            for d in range(2):
                S.op(S.pool, lambda e: e.tensor_copy(nbc2[:, par, d, :], Cring[:, sl, d, 128:129].to_broadcast([128, 128])),
                     reads=[f"C{sl}"], writes=[f"nbc{par}"])

        def stage_Qm(u):
            i, first = u["i"], u["first"]
            lo, hi, dirs, halves = halves_of(i)
            sct, sck, qst, qsk = u["sct"], u["sck"], u["qst"], u["qsk"]
            par = i % 2
            na, nak = ps_acc.next()

            def nummm(e):
                for x, (c, d) in enumerate(halves):
                    xs = slice(x * 128, (x + 1) * 128)
                    ins = e.matmul(na[:, xs], vt[:, c, 0:128], sct[:, xs], start=True, stop=first)
                    if not first:
                        ins = e.matmul(na[:, xs], Cbf2[:, par, d, :], qst[:, xs], start=False, stop=True)
                    ins = e.matmul(na[:, 256 + x * 128:256 + (x + 1) * 128], S.ones_bf[:, :], sct[:, xs], start=True, stop=first)
                    if not first:
                        ins = e.matmul(na[:, 256 + x * 128:256 + (x + 1) * 128], nbc2[:, par, d, :], qst[:, xs], start=False, stop=True)
                return ins
            S.op(S.pe, nummm, reads=[vk, sck, "ones"] + ([] if first else [qsk, f"Cbf{par}", f"nbc{par}"]), writes=[nak])
            rdt, rdk = rdrot.next()
            S.op(S.dve, lambda e: e.tensor_scalar(rdt[:, :], na[:, 256:512], -1.0, 1.0, ALU.mult, ALU.max), reads=[nak], writes=[rdk])
            S.op(S.dve, lambda e: e.tensor_tensor(rdt[:, :], rdt[:, :], na[:, 256:512], ALU.max), reads=[nak, rdk], writes=[rdk])
            S.op(S.dve, lambda e: e.reciprocal(rdt[:, :], rdt[:, :]), reads=[rdk], writes=[rdk])
            HS3 = HS[:, :].rearrange("p (c b) -> p c b", b=128)[:, lo:hi + 1:hi - lo, :]
            r3 = lambda t: t.rearrange("p (a b) -> p a b", a=2)
            if i < NCH // 2:
                S.op(S.dve, lambda e: e.tensor_tensor(HS3, r3(na[:, 0:256]), r3(rdt[:, :]), ALU.mult),
                     reads=[nak, rdk], writes=[f"HS{lo}", f"HS{hi}"])
            else:
                S.op(S.dve, lambda e: e.tensor_tensor(rdt[:, :], na[:, 0:256], rdt[:, :], ALU.mult), reads=[nak, rdk], writes=[rdk])
                S.op(S.pool, lambda e: e.tensor_tensor(HS3, HS3, r3(rdt[:, :]), ALU.add),
                     reads=[rdk, f"HS{lo}", f"HS{hi}"], writes=[f"HS{lo}", f"HS{hi}"])

        us = {}
        for j in range(NCH + 2):
            if j < NCH:
                us[j] = stage_P1(j)
            if 0 <= j - 1 < NCH:
                stage_P2(us[j - 1])
                stage_Qc(us[j - 1])
            if 0 <= j - 2 < NCH:
                stage_Qm(us.pop(j - 2))

        for qc in range(8):
            O, ok = orot.next()
            S.op(S.dve, lambda e: e.tensor_tensor(O[:, :], HS[:, qc * 512:(qc + 1) * 512], F0[:, qc * 512:(qc + 1) * 512], ALU.mult),
                 reads=[f"HS{c}" for c in range(qc * 4, qc * 4 + 4)] + ["F0"], writes=[ok])
            head_norm_store(S, O, ok, 512, hng_sb[:, 5 + h:6 + h], ps_st, sqrot, outT.rearrange("two r t -> r two t")[(5 + h) * 128:(6 + h) * 128], qc * 512,
                            stgrot, t32)


def host_cf():
    r = np.arange(128)
    cf = np.zeros((128, 8, 128), np.float32)
    cf[:, 0, :] = (r[:, None] <= r[None, :])
    cf[:, 1, :] = (r[:, None] >= r[None, :])
    cf[:, 2, :] = np.where(r[:, None] > r[None, :], NEGM, 0.0)
    cf[:, 3, :] = np.where(r[:, None] < r[None, :], NEGM, 0.0)
    cf[:, 4, :] = np.eye(128)
    qs = np.float32(128 ** -0.5)
    cf[:, 5, :] = cf[:, 0, :] * qs
    cf[:, 6, :] = cf[:, 1, :] * qs
    cf[:, 7, :] = cf[:, 0, :] * qs
    return cf


def alibi(n):
    return 2.0 ** (-8.0 * np.arange(1, n + 1) / n)


def strips_A(heads):
    kk = np.arange(128)[:, None].astype(np.float64)
    i = np.arange(WA_LEN)[None, :].astype(np.float64)
    dlt = kk + 1408 - i
    ad = np.abs(dlt)
    cnt = (ad <= 64) * 1.0 + ((ad <= 256) & (np.mod(ad, 4) == 0)) * 1.0 + ((ad <= 1024) & (np.mod(ad, 16) == 0)) * 1.0
    sl = alibi(6)
    return np.stack([cnt * np.exp(-sl[h] * ad) for h in heads]).astype(NPBF)


def strips_B(heads):
    kk = np.arange(128)[:, None].astype(np.float64)
    i = np.arange(WB_LEN)[None, :].astype(np.float64)
    ad = np.abs(kk + 3968 - i)
    sl = alibi(4)
    return np.stack([np.exp(-sl[h] * ad) for h in heads]).astype(NPBF)


def load_wblk(S, wrot, W, r0, kc, c0, ncols):
    wt, wk = wrot.next()
    Wv = W[r0 * 128:(r0 + kc) * 128, :].rearrange("(k p) n -> p k n", p=128)
    step = 4 if ncols > 256 else 8
    for k0 in range(0, kc, step):
        S.dma(S.pool, wt[:, k0:k0 + step, 0:ncols], Wv[:, k0:k0 + step, c0:c0 + ncols], writes=[wk], semkey=wk)
    return wt, wk


def linear_fm(S, wrot, psrot, W, r0, kc, c0, ncols, xT, xk0, xkeys, ntok, evac, wcols=512):
    for cb in range(c0, c0 + ncols, wcols):
        nb = min(wcols, c0 + ncols - cb)
        wt, wk = load_wblk(S, wrot, W, r0, kc, cb, nb)
        for cc in range(nb // 128):
            for t0 in range(0, ntok, 512):
                n = min(512, ntok - t0)
                pt, pk = psrot.next()

                def mm(e):
                    for k in range(kc):
                        ins = e.matmul(pt[:, 0:n], wt[:, k, cc * 128:(cc + 1) * 128], xT[:, xk0 + k, t0:t0 + n],
                                       start=(k == 0), stop=(k == kc - 1))
                    return ins
                S.op(S.pe, mm, reads=[wk] + list(xkeys), writes=[pk])
                evac((cb - c0) // 128 + cc, pt, pk, t0, n)


def phase_C(S, last, hT, headsT, memT, w_out, w_xq, w_xkv, w_xo, w_up, w_down, gcols, hT_out):
    nc = S.nc
    TB = 1024
    WC = 256
    setup_consts(S)
    g_sb = S.sb([128, 4, KC], F32, "g_sb")
    S.dma(S.sp, g_sb[:, :, :], gcols, writes=["g"], semkey="g")
    psrot = Rot(S, 8, [128, 512], F32, "ps", psum=True)
    sqrot = Rot(S, 3, [128, 512], BF16, "sq")
    wrot = Rot(S, 4, [128, KC, WC], BF16, "wbuf")
    rstd = S.sb([128, TB], F32, "rstd")
    hblk = S.sb([128, KC, TB], F32, "hblk")
    X1 = S.sb([128, KC, TB], BF16, "X1")
    X2 = S.sb([128, KC, TB], BF16, "X2")
    kT = S.sb([128, KC, MEM], BF16, "kTx")
    v_sb = S.sb([128, 2, D], BF16, "v_sb")
    prot = Rot(S, 4, [128, 512], BF16, "pp")
    t32 = Rot(S, 4, [128, 512], F32, "t32")
    hkeys = [f"h{j}" for j in range(KC)]
    evn = [0]

    def normalize(gi, dst, dkey, src, skeys, ntok):
        for t0 in range(0, ntok, 512):
            rms_stats_fm(S, src, skeys, KC, min(512, ntok - t0), S.ones_bf, psrot, sqrot, rstd, "rstd", D, t0=t0)
        for j in range(KC):
            S.op(S.dve, lambda e: e.scalar_tensor_tensor(out=dst[:, j, 0:ntok], in0=src[:, j, 0:ntok], scalar=g_sb[:, gi, j:j + 1],
                                                         in1=rstd[:, 0:ntok], op0=ALU.mult, op1=ALU.mult),
                 reads=[skeys[j], "g", "rstd"], writes=[dkey])

    def evac_copy(dst, dkey):
        def f(cc, pt, pk, t0, n):
            evn[0] += 1
            if evn[0] % 2:
                S.op(S.act, lambda e: e.copy(dst[:, cc, t0:t0 + n], pt[:, 0:n]), reads=[pk], writes=[dkey])
            else:
                S.op(S.dve, lambda e: e.tensor_copy(dst[:, cc, t0:t0 + n], pt[:, 0:n]), reads=[pk], writes=[dkey])
        return f

    def evac_addh(cc, pt, pk, t0, n):
        S.op(S.dve, lambda e: e.tensor_tensor(hblk[:, cc, t0:t0 + n], hblk[:, cc, t0:t0 + n], pt[:, 0:n], ALU.add),
             reads=[pk, hkeys[cc]], writes=[hkeys[cc]])

    memf = hblk[:, :, 0:MEM]
    memn = X1[:, :, 0:MEM]
    mkeys = [f"mem{j}" for j in range(KC)]
    memTv = memT.rearrange("(k p) m -> p k m", p=128)
    for k0 in range(0, KC, 4):
        S.dma(S.sp, memf[:, k0:k0 + 4, :], memTv[:, k0:k0 + 4, :], writes=mkeys[k0:k0 + 4], semkey=f"mem{k0}")
    normalize(1, memn, "memn", memf, mkeys, MEM)
    linear_fm(S, wrot, psrot, w_xkv, 0, KC, 0, D, memn, 0, ["memn"], MEM, evac_copy(kT, "kTx"), wcols=WC)
    for cb in range(0, D, WC):
        wt, wk = load_wblk(S, wrot, w_xkv, 0, KC, D + cb, WC)
        for mt in range(2):
            pt, pk = psrot.next()

            def mm(e):
                for k in range(KC):
                    ins = e.matmul(pt[:, 0:WC], memn[:, k, mt * 128:(mt + 1) * 128], wt[:, k, :], start=(k == 0), stop=(k == KC - 1))
                return ins
            S.op(S.pe, mm, reads=[wk, "memn"], writes=[pk])
            S.op(S.act, lambda e: e.copy(v_sb[:, mt, cb:cb + WC], pt[:, 0:WC]), reads=[pk], writes=["v_sb"])
    S.barrier()

    hTv = hT.rearrange("(k p) t -> p k t", p=128)
    hdv = headsT.rearrange("(k p) t -> p k t", p=128)
    hov = hT_out.rearrange("(k p) t -> p k t", p=128)
    for tb in range(T2 // TB):
        ts_ = slice(tb * TB, (tb + 1) * TB)
        for k0 in range(0, KC, 4):
            S.dma(S.sp, hblk[:, k0:k0 + 4, :], hTv[:, k0:k0 + 4, ts_], writes=hkeys[k0:k0 + 4], semkey=f"hblk{k0}")
            S.dma(S.sp, X1[:, k0:k0 + 4, :], hdv[:, k0:k0 + 4, ts_], writes=["X1"], semkey="X1")
        linear_fm(S, wrot, psrot, w_out, 0, KC, 0, D, X1, 0, ["X1"], TB, evac_addh, wcols=WC)
        normalize(0, X1, "X1", hblk, hkeys, TB)
        linear_fm(S, wrot, psrot, w_xq, 0, KC, 0, D, X1, 0, ["X1"], TB, evac_copy(X2, "X2"), wcols=WC)
        for hx in range(4):
            for t0 in range(0, TB, 512):
                ps_ = []
                for mt in range(2):
                    sp_, spk = psrot.next()

                    def mm(e):
                        for dc in range(4):
                            ins = e.matmul(sp_[:, :], kT[:, hx * 4 + dc, mt * 128:(mt + 1) * 128], X2[:, hx * 4 + dc, t0:t0 + 512],
                                           start=(dc == 0), stop=(dc == 3))
                        return ins
                    S.op(S.pe, mm, reads=["kTx", "X2"], writes=[spk])
                    pt_, ptk = prot.next()
                    S.op(S.act, lambda e: e.activation(pt_[:, :], sp_[:, :], AF.Exp, scale=512 ** -0.5), reads=[spk], writes=[ptk])
                    ps_.append((pt_, ptk))
                dn, dnk = psrot.next()

                def mmd(e):
                    e.matmul(dn[:, :], S.ones_bf[:, :], ps_[0][0][:, :], start=True, stop=False)
                    return e.matmul(dn[:, :], S.ones_bf[:, :], ps_[1][0][:, :], start=False, stop=True)
                S.op(S.pe, mmd, reads=["ones", ps_[0][1], ps_[1][1]], writes=[dnk])
                rd, rdk = t32.next()
                S.op(S.dve, lambda e: e.reciprocal(rd[:, :], dn[:, :]), reads=[dnk], writes=[rdk])
                for dvc in range(4):
                    op_, opk = psrot.next()

                    def mmo(e):
                        c_ = hx * 512 + dvc * 128
                        e.matmul(op_[:, :], v_sb[:, 0, c_:c_ + 128], ps_[0][0][:, :], start=True, stop=False)
                        return e.matmul(op_[:, :], v_sb[:, 1, c_:c_ + 128], ps_[1][0][:, :], start=False, stop=True)
                    S.op(S.pe, mmo, reads=["v_sb", ps_[0][1], ps_[1][1]], writes=[opk])
                    S.op(S.dve, lambda e: e.tensor_tensor(X1[:, hx * 4 + dvc, t0:t0 + 512], op_[:, :], rd[:, :], ALU.mult),
                         reads=[opk, rdk], writes=["X1"])
        linear_fm(S, wrot, psrot, w_xo, 0, KC, 0, D, X1, 0, ["X1"], TB, evac_addh, wcols=WC)
        normalize(2, X1, "X1", hblk, hkeys, TB)
        for fq in range(4):
            def evac_act(cc, pt, pk, t0, n):
                tt, tk = t32.next()
                S.op(S.act, lambda e: e.activation(tt[:, 0:n], pt[:, 0:n], AF.Relu), reads=[pk], writes=[tk])
                S.op(S.dve, lambda e: e.tensor_tensor(X2[:, cc, t0:t0 + n], tt[:, 0:n], tt[:, 0:n], ALU.mult), reads=[tk], writes=["X2"])
            linear_fm(S, wrot, psrot, w_up, 0, KC, fq * 2048, 2048, X1, 0, ["X1"], TB, evac_act, wcols=WC)
            linear_fm(S, wrot, psrot, w_down, fq * KC, KC, 0, D, X2, 0, ["X2"], TB, evac_addh, wcols=WC)
        if last:
            for t0 in range(0, TB, 512):
                rms_stats_fm(S, hblk, hkeys, KC, 512, S.ones_bf, psrot, sqrot, rstd, "rstd", D, t0=t0)
            for j in range(KC):
                S.op(S.dve, lambda e: e.scalar_tensor_tensor(out=hblk[:, j, :], in0=hblk[:, j, :], scalar=g_sb[:, 3, j:j + 1],
                                                             in1=rstd[:, :], op0=ALU.mult, op1=ALU.mult),
                     reads=[hkeys[j], "g", "rstd"], writes=[hkeys[j]])
        for k0 in range(0, KC, 4):
            S.out_dma(S.sp, hov[:, k0:k0 + 4, ts_], hblk[:, k0:k0 + 4, :], hkeys[k0:k0 + 4], f"hout{k0}")


PAIRS = [[0, 1], [2, 3], [4, 5], [6, 7]]


def build_fused():
    nc = bass.Bass("TRN2", target_bir_lowering=False)
    di = lambda n, shp, dt: nc.dram_tensor(n, shp, dt, kind="ExternalInput").ap()
    it = lambda n, shp, dt: nc.dram_tensor(n, shp, dt).ap()
    xT = di("xT", [D, T2], F32)
    memT = di("memT", [D, MEM], F32)
    wA = di("wA", [3, 128, WA_LEN], BF16)
    wB = di("wB", [2, 128, WB_LEN], BF16)
    cf = di("cf", [128, 8, 128], F32)
    L = []
    for l in range(DEPTH):
        L.append(dict(
            gmix=di(f"gmix{l}", [128, KC], F32), wfm=di(f"wfm{l}", [D, NFM_BF + NFM_F], F32), wtm=di(f"wtm{l}", [D, NTM + NG], F32),
            convw=di(f"convw{l}", [128, 6, 3], F32), gb=di(f"gb{l}", [128, 12], F32), dl=di(f"dl{l}", [128, 256], F32),
            hng=di(f"hng{l}", [128, 8], F32), w_out=di(f"w_out{l}", [D, D], F32), w_xq=di(f"w_xq{l}", [D, D], F32),
            w_xkv=di(f"w_xkv{l}", [D, 2 * D], F32), w_xo=di(f"w_xo{l}", [D, D], F32), w_up=di(f"w_up{l}", [D, DFF], F32),
            w_down=di(f"w_down{l}", [DFF, D], F32), gcols=di(f"gcols{l}", [128, 4, KC], F32)))
    outT = nc.dram_tensor("outT", [D, T2], F32, kind="ExternalOutput").ap()
    oa_bf, oa_f = it("oa_bf", [NFM_BF, T2], BF16), it("oa_f", [NFM_F, T2], F32)
    oa_t, oa_g = it("oa_t", [2, T2, 1024], BF16), it("oa_g", [2, T2, 12], F32)
    L_bf, L_f = it("L_bf", [2, 2 * 1280 * T2], BF16), it("L_f", [2, 2 * 1152 * T2], F32)
    L_t, ga_g = it("L_t", [2, 2 * T2 * 1024], BF16), it("ga_g", [4 * T2, 12], F32)
    bfin, ffin = it("bfin", [1280, T], BF16), it("ffin", [1152, T], F32)
    tmin, cgin = it("tmin", [T, 1024], BF16), it("cgin", [T, 12], F32)
    ob, L_h, hdin = it("ob", [2, 1024, T2], BF16), it("L_h", [2, 2 * 1024 * T2], BF16), it("hdin", [D, T2], BF16)
    hres = it("hres", [D, T2], F32)
    with ExitStack() as st:
        S = Sched(nc, st)
        g = S.pool.eng

        qengs = (S.pool, S.sp, S.act)
        svals = [E.eng.partition_id() % 2 for E in qengs]
        rr = [0]
        one = lambda ap: ap.rearrange("(o r) t -> o r t", o=1)

        def piece(dst, srcf):
            cc = S.dsem("cc")
            ent = S.dsem("asm")
            qi = rr[0] % 3
            rr[0] += 1
            E = qengs[qi]
            if E.known.get(cc[0].name, 0) < cc[1]:
                E.eng.wait_ge(cc[0], cc[1])
                E.known[cc[0].name] = cc[1]
            E.eng.dma_start(out=dst, in_=srcf(svals[qi])).then_inc(ent[0], 16)
            ent[1] += 16

        def asm(pieces):
            for dst, srcf in pieces:
                piece(dst, srcf)
            S.barrier()

        for l in range(DEPTH):
            P = L[l]
            hin = xT if l == 0 else hres
            S.begin_phase(f"A{l}")
            phase_A(S, hin, P["gmix"], P["wfm"], P["wtm"], oa_bf, oa_f, oa_t, oa_g)
            S.end_phase()
            pieces = []

            def xchg(plane_src, L, nrows, ncols, chunk, dst_view):
                for c0 in range(0, nrows, chunk):
                    n = min(chunk, nrows - c0)
                    off = 2 * c0 * ncols
                    for sp in range(2):
                        S.collective("AllGather", plane_src(sp)[c0:c0 + n, :],
                                     L[sp, off:off + 2 * n * ncols].rearrange("(r t) -> r t", t=ncols), PAIRS)
                    piece(dst_view(c0, n), lambda sv, off=off, n=n: L[bass.ds(sv, 1), off:off + 2 * n * ncols]
                          .rearrange("s (h r t) -> s h r t", h=2, r=n))

            xchg(lambda sp: oa_bf[sp * 1280:(sp + 1) * 1280, :], L_bf, 1280, T2, 512,
                 lambda c0, n: bfin[c0:c0 + n, :].rearrange("(o r) (h t) -> o h r t", o=1, h=2))
            xchg(lambda sp: oa_f[sp * 1152:(sp + 1) * 1152, :], L_f, 1152, T2, 256,
                 lambda c0, n: ffin[c0:c0 + n, :].rearrange("(o r) (h t) -> o h r t", o=1, h=2))
            xchg(lambda sp: oa_t[sp], L_t, T2, 1024, 1024,
                 lambda c0, n: tmin.rearrange("(o h r) t -> o h r t", o=1, h=2)[:, :, c0:c0 + n, :])
            S.collective("AllGather", oa_g.rearrange("a b c -> (a b) c"), ga_g, PAIRS)
            vg = ga_g.rearrange("(h s q a) c -> s h q (a c)", h=2, s=2, q=16)
            pieces.append((cgin.rearrange("(o h q a) c -> o h q (a c)", o=1, h=2, q=16), lambda sv: vg[bass.ds(sv, 1), :, :, :]))
            asm(pieces)
            S.begin_phase(f"B{l}")
            phase_B(S, l, bfin, ffin, tmin, cgin, wA, wB, P["convw"], P["gb"], P["dl"], P["hng"], cf, ob)
            S.end_phase()
            pieces = []
            xchg(lambda hf: ob[hf], L_h, 1024, T2, 512,
                 lambda c0, n: hdin.rearrange("(o sp r) t -> o sp r t", o=1, sp=2)[:, :, c0:c0 + n, :])
            asm(pieces)
            S.begin_phase(f"C{l}")
            phase_C(S, l == DEPTH - 1, hin, hdin, memT, P["w_out"], P["w_xq"], P["w_xkv"], P["w_xo"], P["w_up"], P["w_down"],
                    P["gcols"], hres if l < DEPTH - 1 else outT)
            S.end_phase()
    return nc


_OFF = np.cumsum([0, 768, 768, 768, 512, 512, 512, 1536, 768, 768, 24])
_PROG = {}


def _heads(s):
    return [3 * s + i for i in range(3)], [2 * s + i for i in range(2)]


def _perm_cols():
    o = _OFF
    rng = lambda base, h: list(range(base + h * 128, base + (h + 1) * 128))
    fm_bf, fm_f, tm, gt = [], [], [], []
    for s in range(2):
        hA, hB = _heads(s)
        for base, hs in ((o[0], hA), (o[1], hA), (o[3], hB), (o[4], hB)):
            for h in hs:
                fm_bf += rng(base, h)
        for base, hs in ((o[6], hA), (o[6] + 768, hA), (o[8], hA)):
            for h in hs:
                fm_f += rng(base, h)
        for base, hs in ((o[2], hA), (o[5], hB), (o[7], hA)):
            for h in hs:
                tm += rng(base, h)
        gt += [o[9] + t * 6 + h for t in range(4) for h in hA]
    return np.array(fm_bf + fm_f), np.array(tm + gt)


def _perm_wout_rows():
    rows = []
    for s in range(2):
        hA, hB = _heads(s)
        for h in hA:
            rows += list(range(h * 128, (h + 1) * 128))
        for h in hB:
            rows += list(range(768 + h * 128, 768 + (h + 1) * 128))
        for h in hA:
            rows += list(range(1280 + h * 128, 1280 + (h + 1) * 128))
    return np.array(rows)


def _gcol(g):
    return np.ascontiguousarray(np.asarray(g, np.float32).reshape(KC, 128).T)


def kernel(x, mem, norm_mix_g, w_in, conv_w, gate_b, diff_lambda, head_norm_g, w_out, norm_x_g, norm_mem_g,
           w_xq, w_xkv, w_xo, norm_mlp_g, w_up, w_down, final_norm_g):
    f = lambda a: np.asarray(a, np.float32)
    x, mem, w_in = f(x), f(mem), f(w_in)
    if "nc" not in _PROG:
        _PROG["nc"] = build_fused()
    nc = _PROG["nc"]
    cfm, ctm = _perm_cols()
    wrows = _perm_wout_rows()
    shared = {"cf": host_cf()}
    per_s = []
    for l in range(DEPTH):
        shared[f"gmix{l}"] = _gcol(f(norm_mix_g)[l])
        shared[f"wfm{l}"] = np.ascontiguousarray(w_in[l][:, cfm])
        shared[f"wtm{l}"] = np.ascontiguousarray(w_in[l][:, ctm])
        shared[f"dl{l}"] = np.ascontiguousarray(np.broadcast_to(f(diff_lambda)[l].reshape(1, 256), (128, 256)))
        shared[f"w_out{l}"] = np.ascontiguousarray(f(w_out)[l][wrows, :])
        shared[f"w_xq{l}"] = np.ascontiguousarray(f(w_xq)[l])
        shared[f"w_xkv{l}"] = np.ascontiguousarray(f(w_xkv)[l])
        shared[f"w_xo{l}"] = np.ascontiguousarray(f(w_xo)[l])
        shared[f"w_up{l}"] = np.ascontiguousarray(f(w_up)[l])
        shared[f"w_down{l}"] = np.ascontiguousarray(f(w_down)[l])
        shared[f"gcols{l}"] = np.ascontiguousarray(
            np.stack([f(norm_x_g)[l], f(norm_mem_g)[l], f(norm_mlp_g)[l], f(final_norm_g)]).reshape(4, KC, 128).transpose(2, 0, 1))
    for s in range(2):
        hA, hB = _heads(s)
        d = {"wA": strips_A(hA), "wB": strips_B(hB)}
        for l in range(DEPTH):
            cw = f(conv_w)[l]
            gbl = f(gate_b)[l]
            hng_l = f(head_norm_g)[l].reshape(16, 128)
            gsel = [t * 6 + h for t in range(4) for h in hA]
            convw = np.stack([cw[:, ch * 128:(ch + 1) * 128] for ch in (hA + [6 + h for h in hA])], axis=1)
            hng = np.stack([hng_l[h] for h in hA] + [hng_l[6 + h] for h in hB] + [hng_l[10 + h] for h in hA], axis=1)
            d[f"convw{l}"] = np.ascontiguousarray(convw.transpose(2, 1, 0))
            d[f"gb{l}"] = np.ascontiguousarray(np.broadcast_to(gbl[gsel][None, :], (128, 12)))
            d[f"hng{l}"] = np.ascontiguousarray(hng)
        per_s.append(d)
    in_maps = []
    for c in range(8):
        b, s = c // 2, c % 2
        m = dict(shared)
        m.update(per_s[s])
        m["xT"] = np.ascontiguousarray(x[b, s * T2:(s + 1) * T2, :].T)
        m["memT"] = np.ascontiguousarray(mem[b].T)
        in_maps.append(m)
    res = run_bass_kernel_spmd(nc, in_maps, core_ids=list(range(8))).results
    out = np.empty((4, T, D), np.float32)
    for c in range(8):
        out[c // 2, (c % 2) * T2:(c % 2 + 1) * T2, :] = res[c]["outT"].T
    return out
```
